# Optimizing a Trainium2 kernel written in Bass

```python
import jax
import jax.numpy as jnp
from jax import lax
import numpy as np


D_MODEL = 4096
BATCH = 2
SEQ = 8192
DEPTH = 4

GRID_W = 64
CTX_LEN = 256
CHUNK = 64
EPS = 1e-6
F32 = jnp.float32

A_W = 3 * D_MODEL // 8
A_DK = 128
A_DV = 128
A_HEADS = A_W // A_DV
A_K = A_HEADS * A_DK
B_W = D_MODEL // 4
B_BW = 128
B_BLOCKS = B_W // B_BW
B_CONV = 4
RG_C = 8.0
C_W = D_MODEL - A_W - B_W
C_HEADS = 4
C_DV = C_W // C_HEADS
C_DK = C_DV // 2
C_K = C_HEADS * C_DK
C_RANK = 16
GLA_NORMALIZER = 16.0
IN_SIZES = (A_K, A_K, A_K, A_W, A_W, B_W, B_W, C_K, C_K, C_W, C_W, C_RANK, C_RANK)
IN_DIM = 3 * A_K + 2 * A_W + 2 * B_W + 2 * C_K + 2 * C_W + 2 * C_RANK
FFN_DIM = D_MODEL
FFN_CONV = 3

kernel_name = 'hybrid_hgrn2_rglru_gla_prefix_dit'


def rms_norm(x, g):
    xf = x.astype(F32)
    y = xf * lax.rsqrt(jnp.mean(xf * xf, axis=-1, keepdims=True) + EPS)
    return (y * g.astype(F32)).astype(x.dtype)


def head_rms(o, gain):
    Bn, L, H, V = o.shape
    y = o * lax.rsqrt(jnp.mean(o * o, axis=-1, keepdims=True) + EPS)
    return y.reshape(Bn, L, H * V) * gain.astype(F32)


def modulate(h, shift, scale):
    return h * (1.0 + scale) + shift


def split_cols(p, sizes):
    outs, off = [], 0
    for s in sizes:
        outs.append(p[..., off:off + s])
        off += s
    return outs


def flip(t):
    return jnp.flip(t, axis=1)


def chunk_gla(q, k, v, logf, s0):
    Bn, L, H, K = q.shape
    V = v.shape[-1]
    n = L // CHUNK

    def to_chunks(t):
        return t.astype(F32).reshape(Bn, n, CHUNK, H, t.shape[-1]).transpose(1, 0, 3, 2, 4)

    pos = jnp.arange(CHUNK)
    mask = (pos[:, None] >= pos[None, :])[:, :, None]
    maskf = mask.astype(F32)

    def step(S, inp):
        qc, kc, vc, gc = inp
        b = jnp.cumsum(gc, axis=2)
        diff = b[:, :, :, None, :] - b[:, :, None, :, :]
        dec = jnp.exp(jnp.where(mask, diff, 0.0)) * maskf
        att = jnp.einsum('bhtk,bhsk,bhtsk->bhts', qc, kc, dec)
        o = jnp.einsum('bhts,bhsv->bhtv', att, vc) + jnp.einsum('bhtk,bhkv->bhtv', qc * jnp.exp(b), S)
        b_last = b[:, :, -1:, :]
        S = jnp.exp(b_last[:, :, 0, :, None]) * S + jnp.einsum('bhsk,bhsv->bhkv', kc * jnp.exp(b_last - b), vc)
        return S, o

    S, o = lax.scan(step, s0.astype(F32), (to_chunks(q), to_chunks(k), to_chunks(v), to_chunks(logf)))
    o = o.transpose(1, 0, 3, 2, 4).reshape(Bn, L, H, V)
    return o, S


def two_way_gla(c_args, l_args):
    qc, kfc, kbc, vc, gfc, gbc = c_args
    ql, kfl, kbl, vl, gfl, gbl = l_args
    Bn, _, H, K = qc.shape
    V = vc.shape[-1]
    s0 = jnp.zeros((Bn, H, K, V), F32)
    oc_f, s_f = chunk_gla(qc, kfc, vc, gfc, s0)
    oc_b, s_b = chunk_gla(flip(qc), flip(kbc), flip(vc), flip(gbc), s0)
    ol_f, _ = chunk_gla(ql, kfl, vl, gfl, s_f)
    ol_b, _ = chunk_gla(flip(ql), flip(kbl), flip(vl), flip(gbl), s_b)
    return oc_f + flip(oc_b), ol_f + flip(ol_b)


def hgrn_forget(z, lb):
    f = lb + (1.0 - lb) * jax.nn.sigmoid(z)
    log_f = jnp.log(f)
    key = (1.0 - lb) * jax.nn.sigmoid(-z)
    return log_f, key


def hgrn2_group(pc, pl, lb_f, lb_b, gain, need_ctx):
    lbf = lb_f.reshape(A_HEADS, A_DK)
    lbb = lb_b.reshape(A_HEADS, A_DK)

    def prep(parts):
        q, zf, zb, i, _ = parts
        Bn, L, _ = q.shape
        heads = lambda t: t.astype(F32).reshape(Bn, L, A_HEADS, A_DK)
        qh = heads(jax.nn.silu(q)) * (A_DK ** -0.5)
        logf_f, k_f = hgrn_forget(heads(zf), lbf)
        logf_b, k_b = hgrn_forget(heads(zb), lbb)
        v = i.astype(F32).reshape(Bn, L, A_HEADS, A_DV)
        return (qh, k_f, k_b, v, logf_f, logf_b)

    oc, ol = two_way_gla(prep(pc), prep(pl))
    finish = lambda o, g: head_rms(o, gain) * jax.nn.silu(g.astype(F32))
    yl = finish(ol, pl[4])
    yc = finish(oc, pc[4]) if need_ctx else None
    return yc, yl


def gla_group(pc, pl, wa_f, ba_f, wa_b, ba_b, gain, need_ctx):
    def prep(parts):
        q, k, v, _, af, ab = parts
        Bn, L, _ = q.shape
        heads = lambda t: t.astype(F32).reshape(Bn, L, C_HEADS, C_DK)
        qh = heads(q) * (C_DK ** -0.5)
        kh = heads(k)
        vh = v.astype(F32).reshape(Bn, L, C_HEADS, C_DV)
        gf = heads(jax.nn.log_sigmoid(af.astype(F32) @ wa_f.astype(F32) + ba_f.astype(F32)) / GLA_NORMALIZER)
        gb = heads(jax.nn.log_sigmoid(ab.astype(F32) @ wa_b.astype(F32) + ba_b.astype(F32)) / GLA_NORMALIZER)
        return (qh, kh, kh, vh, gf, gb)

    oc, ol = two_way_gla(prep(pc), prep(pl))
    finish = lambda o, g: head_rms(o, gain) * jax.nn.silu(g.astype(F32))
    yl = finish(ol, pl[3])
    yc = finish(oc, pc[3]) if need_ctx else None
    return yc, yl


def dwconv1d(x, w, b):
    y = lax.conv_general_dilated(
        x, w[:, None, :], window_strides=(1,),
        padding=[(B_CONV // 2, B_CONV - 1 - B_CONV // 2)],
        dimension_numbers=('NWC', 'WIO', 'NWC'), feature_group_count=x.shape[-1])
    return y + b


def rglru_scan(x, w_r, b_r, w_i, b_i, lam, h0):
    Bn, L, W = x.shape
    xb = x.reshape(Bn, L, B_BLOCKS, B_BW)
    r = jax.nn.sigmoid(jnp.einsum('blni,nij->blnj', xb, w_r.astype(F32)).reshape(Bn, L, W) + b_r.astype(F32))
    i = jax.nn.sigmoid(jnp.einsum('blni,nij->blnj', xb, w_i.astype(F32)).reshape(Bn, L, W) + b_i.astype(F32))
    log_a = -RG_C * jax.nn.softplus(-lam.astype(F32)) * r
    a = jnp.exp(log_a)
    u = jnp.sqrt(jnp.maximum(-jnp.expm1(2.0 * log_a), 0.0)) * (i * x)
    u = u.at[:, 0].add(a[:, 0] * h0)

    def combine(left, right):
        a1, u1 = left
        a2, u2 = right
        return a1 * a2, a2 * u1 + u2

    _, h = lax.associative_scan(combine, (a, u), axis=1)
    return h


def rglru_group(pc, pl, conv_w, conv_b, fwd, bwd, gain, need_ctx):
    xc = dwconv1d(pc[0], conv_w, conv_b).astype(F32)
    xl = dwconv1d(pl[0], conv_w, conv_b).astype(F32)
    h0 = jnp.zeros((xc.shape[0], B_W), F32)
    hc_f = rglru_scan(xc, *fwd, h0)
    hc_b = flip(rglru_scan(flip(xc), *bwd, h0))
    hl = rglru_scan(xl, *fwd, hc_f[:, -1]) + flip(rglru_scan(flip(xl), *bwd, hc_b[:, 0]))
    finish = lambda h, gate: rms_norm(h, gain) * jax.nn.gelu(gate.astype(F32))
    yl = finish(hl, pl[1])
    yc = finish(hc_f + hc_b, pc[1]) if need_ctx else None
    return yc, yl


def conv_ffn(h, w_up, conv_w, conv_b, w_down, rows, cols):
    Bn, L, _ = h.shape
    up = h @ w_up
    u, gt = up[..., :FFN_DIM], up[..., FFN_DIM:]
    gt = gt.reshape(Bn, rows, cols, FFN_DIM)
    gt = lax.conv_general_dilated(
        gt, conv_w[:, :, None, :], window_strides=(1, 1), padding='SAME',
        dimension_numbers=('NHWC', 'HWIO', 'NHWC'), feature_group_count=FFN_DIM) + conv_b
    act = jax.nn.silu(gt.reshape(Bn, L, FFN_DIM)) * u
    return act @ w_down


def setup_inputs(seed: int = 0) -> dict:
    key = jax.random.key(seed)
    ks = iter(jax.random.split(key, 48))
    D = D_MODEL

    def nrm(shape, scale):
        return jax.random.normal(next(ks), shape, F32) * scale

    def gain(shape):
        return 1.0 + nrm(shape, 0.05)

    def lam_init():
        a0 = jax.random.uniform(next(ks), (DEPTH, B_W), F32, 0.9, 0.999)
        p = a0 ** (1.0 / RG_C)
        return jnp.log(p) - jnp.log1p(-p)

    return {
        'x': nrm((BATCH, SEQ, D), 1.0),
        'c': nrm((BATCH, D), 1.0),
        'ctx': nrm((BATCH, CTX_LEN, D), 1.0),
        'c_ctx': nrm((D,), 1.0),
        'w_ada': nrm((DEPTH, D, 6 * D), 0.2 * D ** -0.5),
        'b_ada': nrm((DEPTH, 6 * D), 0.02),
        'norm1_g': gain((DEPTH, D)),
        'w_in': nrm((DEPTH, D, IN_DIM), D ** -0.5),
        'hgrn_lb_fwd': nrm((DEPTH, A_K), 0.5),
        'hgrn_lb_bwd': nrm((DEPTH, A_K), 0.5),
        'hgrn_norm_g': gain((DEPTH, A_W)),
        'lru_conv_w': nrm((DEPTH, B_CONV, B_W), B_CONV ** -0.5),
        'lru_conv_b': nrm((DEPTH, B_W), 0.02),
        'lru_wr_fwd': nrm((DEPTH, B_BLOCKS, B_BW, B_BW), B_BW ** -0.5),
        'lru_br_fwd': nrm((DEPTH, B_W), 0.1),
        'lru_wi_fwd': nrm((DEPTH, B_BLOCKS, B_BW, B_BW), B_BW ** -0.5),
        'lru_bi_fwd': nrm((DEPTH, B_W), 0.1),
        'lru_lam_fwd': lam_init(),
        'lru_wr_bwd': nrm((DEPTH, B_BLOCKS, B_BW, B_BW), B_BW ** -0.5),
        'lru_br_bwd': nrm((DEPTH, B_W), 0.1),
        'lru_wi_bwd': nrm((DEPTH, B_BLOCKS, B_BW, B_BW), B_BW ** -0.5),
        'lru_bi_bwd': nrm((DEPTH, B_W), 0.1),
        'lru_lam_bwd': lam_init(),
        'lru_norm_g': gain((DEPTH, B_W)),
        'gla_wa_fwd': nrm((DEPTH, C_RANK, C_K), C_RANK ** -0.5),
        'gla_ba_fwd': nrm((DEPTH, C_K), 0.1),
        'gla_wa_bwd': nrm((DEPTH, C_RANK, C_K), C_RANK ** -0.5),
        'gla_ba_bwd': nrm((DEPTH, C_K), 0.1),
        'gla_norm_g': gain((DEPTH, C_W)),
        'w_out': nrm((DEPTH, A_W + B_W + C_W, D), (A_W + B_W + C_W) ** -0.5),
        'norm2_g': gain((DEPTH, D)),
        'w_up': nrm((DEPTH, D, 2 * FFN_DIM), D ** -0.5),
        'ffn_conv_w': nrm((DEPTH, FFN_CONV, FFN_CONV, FFN_DIM), 1.0 / FFN_CONV),
        'ffn_conv_b': nrm((DEPTH, FFN_DIM), 0.02),
        'w_down': nrm((DEPTH, FFN_DIM, D), FFN_DIM ** -0.5),
        'final_norm_g': gain((D,)),
    }


def reference(x, c, ctx, c_ctx, w_ada, b_ada, norm1_g, w_in, hgrn_lb_fwd, hgrn_lb_bwd, hgrn_norm_g,
              lru_conv_w, lru_conv_b, lru_wr_fwd, lru_br_fwd, lru_wi_fwd, lru_bi_fwd, lru_lam_fwd,
              lru_wr_bwd, lru_br_bwd, lru_wi_bwd, lru_bi_bwd, lru_lam_bwd, lru_norm_g,
              gla_wa_fwd, gla_ba_fwd, gla_wa_bwd, gla_ba_bwd, gla_norm_g, w_out, norm2_g,
              w_up, ffn_conv_w, ffn_conv_b, w_down, final_norm_g):
    rows = x.shape[1] // GRID_W

    def lower_bounds(logits):
        p = jax.nn.softmax(logits.astype(F32), axis=0)
        return jnp.clip(jnp.cumsum(p, axis=0) - p[0], 0.0, 1.0)

    lbf_all = lower_bounds(hgrn_lb_fwd)
    lbb_all = lower_bounds(hgrn_lb_bwd)
    silu_c = jax.nn.silu(c)
    silu_cc = jax.nn.silu(c_ctx)
    xl, xc = x, ctx
    for l in range(DEPTH):
        need_ctx = l < DEPTH - 1
        mod_l = (silu_c @ w_ada[l] + b_ada[l])[:, None, :]
        mod_c = silu_cc @ w_ada[l] + b_ada[l]
        sh1, sc1, g1, sh2, sc2, g2 = jnp.split(mod_l, 6, axis=-1)
        csh1, csc1, cg1, csh2, csc2, cg2 = jnp.split(mod_c, 6, axis=-1)

        hl = modulate(rms_norm(xl, norm1_g[l]), sh1, sc1)
        hc = modulate(rms_norm(xc, norm1_g[l]), csh1, csc1)
        pl = split_cols(hl @ w_in[l], IN_SIZES)
        pc = split_cols(hc @ w_in[l], IN_SIZES)
        ya_c, ya_l = hgrn2_group(pc[0:5], pl[0:5], lbf_all[l], lbb_all[l], hgrn_norm_g[l], need_ctx)
        fwd = (lru_wr_fwd[l], lru_br_fwd[l], lru_wi_fwd[l], lru_bi_fwd[l], lru_lam_fwd[l])
        bwd = (lru_wr_bwd[l], lru_br_bwd[l], lru_wi_bwd[l], lru_bi_bwd[l], lru_lam_bwd[l])
        yb_c, yb_l = rglru_group(pc[5:7], pl[5:7], lru_conv_w[l], lru_conv_b[l], fwd, bwd, lru_norm_g[l], need_ctx)
        yc_c, yc_l = gla_group(pc[7:13], pl[7:13], gla_wa_fwd[l], gla_ba_fwd[l], gla_wa_bwd[l], gla_ba_bwd[l],
                               gla_norm_g[l], need_ctx)
        y_l = jnp.concatenate([ya_l, yb_l, yc_l], axis=-1).astype(x.dtype)
        xl = xl + g1 * (y_l @ w_out[l])
        hl2 = modulate(rms_norm(xl, norm2_g[l]), sh2, sc2)
        xl = xl + g2 * conv_ffn(hl2, w_up[l], ffn_conv_w[l], ffn_conv_b[l], w_down[l], rows, GRID_W)

        if need_ctx:
            y_c = jnp.concatenate([ya_c, yb_c, yc_c], axis=-1).astype(ctx.dtype)
            xc = xc + cg1 * (y_c @ w_out[l])
            hc2 = modulate(rms_norm(xc, norm2_g[l]), csh2, csc2)
            xc = xc + cg2 * conv_ffn(hc2, w_up[l], ffn_conv_w[l], ffn_conv_b[l], w_down[l], 1, xc.shape[1])

    return rms_norm(xl, final_norm_g)
```

```python
import contextlib
import os
import numpy as np
import ml_dtypes
import concourse.bass as bass
import concourse.mybir as mybir
from concourse.bass_utils import run_bass_kernel_spmd

F32 = mybir.dt.float32
BF16 = mybir.dt.bfloat16
AF = mybir.ActivationFunctionType
ALU = mybir.AluOpType

D = 4096
NCH = 32
BATCH = 2
SEQ = 8192
CTX = 256
T_ALL = SEQ + CTX
DEPTH = 4
EPS = 1e-6
NCORES = 8


class Buf:
    __slots__ = ("name", "writers", "readers", "disjoint", "key")

    def __init__(self, name, disjoint=False, key=None):
        self.name = name
        self.writers = []
        self.readers = []
        self.disjoint = disjoint
        self.key = key or name


class Op:
    __slots__ = ("eng", "fn", "dma", "deps", "sig", "sem", "val", "key", "idx")


class Prog:
    ENGS = ("pe", "act", "dve", "pool", "sp")

    def __init__(self):
        self.ops = []
        self.per_eng = {e: [] for e in self.ENGS}
        self.dma_keys = {}
        self.last_dma = {}

    def op(self, eng, fn, reads=(), writes=(), dma=False, key=None, extra_deps=()):
        o = Op()
        o.eng, o.fn, o.dma, o.sig, o.sem, o.val, o.key = eng, fn, dma, False, None, 0, None
        o.idx = len(self.ops)
        deps = {}
        for b in reads:
            for w in b.writers:
                deps[w.idx] = (w, True)
        for b in writes:
            for r in b.readers:
                if r.idx not in deps:
                    deps[r.idx] = (r, False)
            if not (b.disjoint and not b.readers):
                for w in b.writers:
                    if w.idx not in deps:
                        deps[w.idx] = (w, False)
        for d in extra_deps:
            deps[d.idx] = (d, True)
        final = []
        for d, raw in deps.values():
            if d is o:
                continue
            if (not dma) and (not d.dma) and d.eng == eng:
                if eng == "pe" or not raw:
                    continue
            final.append(d)
            d.sig = True
        o.deps = final
        for b in reads:
            if b not in writes:
                b.readers.append(o)
        for b in writes:
            if b.disjoint and not b.readers:
                b.writers.append(o)
            else:
                b.writers = [o]
                b.readers = []
        if dma:
            if key is None:
                key = writes[0].key if writes else "misc"
            o.key = key
            o.sig = True
            self.dma_keys.setdefault(key, 0)
            self.last_dma[key] = o
        self.ops.append(o)
        self.per_eng[eng].append(o)
        return o

    def barrier(self):
        lasts = [self.per_eng[e][-1] for e in self.ENGS if self.per_eng[e]]
        lasts = [o for o in lasts if not o.dma] + list(self.last_dma.values())
        for e in self.ENGS:
            self.op(e, lambda eng: eng.nop(), extra_deps=[d for d in lasts])

    def emit(self, nc, final_keys=()):
        with contextlib.ExitStack() as st:
            esem = {e: st.enter_context(nc.semaphore("s_" + e)) for e in self.ENGS}
            dsem = {k: st.enter_context(nc.semaphore("d%d" % i)) for i, k in enumerate(self.dma_keys)}
            cnt = {e: 0 for e in self.ENGS}
            dcnt = {k: 0 for k in self.dma_keys}
            for o in self.ops:
                if o.dma:
                    dcnt[o.key] += 16
                    o.sem, o.val = dsem[o.key], dcnt[o.key]
                elif o.sig:
                    cnt[o.eng] += 1
                    o.sem, o.val = esem[o.eng], cnt[o.eng]
            finals = [(dsem[k], dcnt[k]) for k in final_keys if dcnt.get(k, 0) > 0]
            block = st.enter_context(nc.Block())
            per_eng = self.per_eng

            def run(engname, eng, extra=None):
                known = {}
                for o in per_eng[engname]:
                    for d in o.deps:
                        sid = id(d.sem)
                        if known.get(sid, 0) >= d.val:
                            continue
                        known[sid] = d.val
                        eng.wait_ge(d.sem, d.val)
                    ins = o.fn(eng)
                    if o.sig:
                        ins.then_inc(o.sem, 16 if o.dma else 1)
                if extra:
                    for s, v in extra:
                        eng.wait_ge(s, v)

            @block.tensor
            def _(e):
                run("pe", e)

            @block.scalar
            def _(e):
                run("act", e)

            @block.vector
            def _(e):
                run("dve", e)

            @block.gpsimd
            def _(e):
                run("pool", e)

            @block.sync
            def _(e):
                run("sp", e, finals)
        return nc


class K:
    def __init__(self, arena_bytes=200 * 1024):
        self.nc = bass.Bass("TRN2", target_bir_lowering=False)
        self.P = Prog()
        self.st = contextlib.ExitStack()
        self.ar = self.st.enter_context(self.nc.sbuf_tensor("arena", [128, arena_bytes // 2], BF16))
        self.cap = arena_bytes
        self.off = 0
        self.banks = [self.st.enter_context(self.nc.psum_tensor("bank%d" % i, [128, 512], F32)) for i in range(7)]
        self.bank_bf = self.st.enter_context(self.nc.psum_tensor("bankbf", [128, 1024], BF16))
        self.nbuf = 0

    def alloc(self, free_shape, dt):
        esz = 4 if dt == F32 else 2
        n = int(np.prod(free_shape))
        nb = (n * esz + 31) // 32 * 32
        assert self.off + nb <= self.cap, ("SBUF arena overflow", self.off, nb, self.cap)
        v = self.ar[:, self.off // 2:(self.off + n * esz) // 2]
        if dt == F32:
            v = v.bitcast(F32)
        self.off += nb
        if len(free_shape) == 2:
            v = v.rearrange("p (a b) -> p a b", a=free_shape[0])
        elif len(free_shape) == 3:
            v = v.rearrange("p (a b c) -> p a b c", a=free_shape[0], b=free_shape[1])
        return v

    def buf(self, name=None, **kw):
        self.nbuf += 1
        return Buf(name or ("b%d" % self.nbuf), **kw)

    def bufs(self, n, name="b", **kw):
        return [self.buf("%s%d" % (name, i), **kw) for i in range(n)]

    def dram_in(self, name, shape, dt=F32):
        return self.nc.dram_tensor(name, list(shape), dt, kind="ExternalInput").ap()

    def dram_out(self, name, shape, dt=F32):
        return self.nc.dram_tensor(name, list(shape), dt, kind="ExternalOutput").ap()

    def dram_tmp(self, name, shape, dt=F32):
        return self.nc.dram_tensor(name, list(shape), dt, kind="Internal").ap()

    def mm(self, out, lhsT, rhs, start=True, stop=True, r=(), w=()):
        return self.P.op("pe", lambda e: e.matmul(out, lhsT, rhs, start=start, stop=stop), reads=r, writes=w)

    def tr(self, out, in_, ident, r=(), w=()):
        return self.P.op("pe", lambda e: e.transpose(out, in_, ident), reads=r, writes=w)

    def act(self, out, in_, func, bias=None, scale=1.0, r=(), w=()):
        def f(e):
            if bias is not None:
                return e.activation(out=out, in_=in_, func=func, bias=bias, scale=scale)
            return e.activation(out=out, in_=in_, func=func, scale=scale)
        return self.P.op("act", f, reads=r, writes=w)

    def tt(self, eng, out, a, b, op, r=(), w=()):
        return self.P.op(eng, lambda e: e.tensor_tensor(out=out, in0=a, in1=b, op=op), reads=r, writes=w)

    def ts(self, eng, out, a, s1, op0, s2=None, op1=None, r=(), w=()):
        def f(e):
            if op1 is None:
                return e.tensor_scalar(out=out, in0=a, scalar1=s1, scalar2=None, op0=op0)
            return e.tensor_scalar(out=out, in0=a, scalar1=s1, scalar2=s2, op0=op0, op1=op1)
        return self.P.op(eng, f, reads=r, writes=w)

    def stt(self, eng, out, a, s, b, op0, op1, r=(), w=()):
        return self.P.op(eng, lambda e: e.scalar_tensor_tensor(out=out, in0=a, scalar=s, in1=b, op0=op0, op1=op1),
                         reads=r, writes=w)

    def scan(self, out, d0, d1, init, r=(), w=()):
        return self.P.op("dve", lambda e: e.tensor_tensor_scan(out=out, data0=d0, data1=d1, initial=init,
                                                              op0=ALU.mult, op1=ALU.add), reads=r, writes=w)

    def cp(self, eng, out, in_, r=(), w=()):
        if eng == "act":
            return self.P.op("act", lambda e: e.copy(out=out, in_=in_), reads=r, writes=w)
        return self.P.op(eng, lambda e: e.tensor_copy(out=out, in_=in_), reads=r, writes=w)

    def recip(self, out, in_, r=(), w=()):
        return self.P.op("dve", lambda e: e.reciprocal(out=out, in_=in_), reads=r, writes=w)

    def memset(self, eng, out, val, r=(), w=()):
        return self.P.op(eng, lambda e: e.memset(out, val), reads=r, writes=w)

    def dma(self, q, out, in_, r=(), w=(), key=None):
        return self.P.op(q, lambda e: e.dma_start(out=out, in_=in_), reads=r, writes=w, dma=True, key=key)

    def finish(self, final_keys):
        self.P.emit(self.nc, final_keys=final_keys)
        self.st.close()
        return self.nc


class Rot:
    def __init__(self, k, n, free_shape, dt, name):
        self.slots = [(k.alloc(free_shape, dt), k.buf("%s%d" % (name, i))) for i in range(n)]
        self.i = 0

    def next(self):
        s = self.slots[self.i % len(self.slots)]
        self.i += 1
        return s


def consts(k):
    ones = k.alloc((128,), BF16)
    b_ones = k.buf("ones")
    k.memset("pool", ones, 1.0, w=[b_ones])
    eps = k.alloc((1,), F32)
    b_eps = k.buf("eps")
    k.memset("pool", eps, EPS, w=[b_eps])
    return ones, b_ones, eps, b_eps


def norm_mod_dram(k, src, t0, tn, Acol, Bcol, rAB, dsts, cst, xrot, sqrot, strot, orot, bank, b_bank, src_r, dst_w):
    ones, b_ones, eps, b_eps = cst
    ps = bank[:, 0:tn]
    for c in range(NCH):
        xt, bx = xrot.next()
        k.dma("sp", xt[:, 0:tn], src[c, :, t0:t0 + tn], r=src_r, w=[bx])
        sq, bs = sqrot.next()
        k.act(sq[:, 0:tn], xt[:, 0:tn], AF.Square, r=[bx], w=[bs])
        k.mm(ps, ones, sq[:, 0:tn], start=(c == 0), stop=(c == NCH - 1), r=[bs, b_ones], w=[b_bank])
    rs, brs = strot.next()
    k.act(rs[:, 0:tn], ps, AF.Sqrt, bias=eps, scale=1.0 / D, r=[b_bank, b_eps], w=[brs])
    k.recip(rs[:, 0:tn], rs[:, 0:tn], r=[brs], w=[brs])
    for c in range(NCH):
        xt, bx = xrot.next()
        k.dma("sp", xt[:, 0:tn], src[c, :, t0:t0 + tn], r=src_r, w=[bx])
        k.stt("dve", xt[:, 0:tn], xt[:, 0:tn], Acol(c), rs[:, 0:tn], ALU.mult, ALU.mult, r=[bx, brs] + rAB, w=[bx])
        for (dst, dt0, dt, rot) in dsts:
            ot, bo = rot.next()
            k.act(ot[:, 0:tn], xt[:, 0:tn], AF.Identity, bias=Bcol(c), scale=1.0, r=[bx] + rAB, w=[bo])
            k.dma("pool", dst[c, :, dt0:dt0 + tn], ot[:, 0:tn], r=[bo], w=dst_w)


def load_small(k, name, shape, dt=F32):
    d = k.dram_in(name, [128] + list(shape), dt)
    v = k.alloc(tuple(shape), dt)
    b = k.buf(name)
    k.dma("sp", v, d, w=[b])
    return v, b


def make_AB(k, mod, bmod, g, bg, isc, ish):
    A = k.alloc((NCH,), F32)
    bA = k.buf()
    k.stt("dve", A, mod[:, isc, :], 1.0, g, ALU.add, ALU.mult, r=[bmod, bg], w=[bA])
    return A, bA, mod[:, ish, :], bmod


TO = 2048 + 64


def build_norm0():
    k = K()
    cst = consts(k)
    xT = k.dram_in("xT", [NCH, 128, TO])
    hT = k.dram_out("hT", [NCH, 128, TO], BF16)
    ml, bml = load_small(k, "mod_lat", [6, NCH])
    mc, bmc = load_small(k, "mod_ctx", [6, NCH])
    g, bg = load_small(k, "gn", [NCH])
    Al, bAl, Bl, bBl = make_AB(k, ml, bml, g, bg, 1, 0)
    Ac, bAc, Bc, bBc = make_AB(k, mc, bmc, g, bg, 1, 0)
    xrot = Rot(k, 4, (512,), F32, "x")
    sqrot = Rot(k, 2, (512,), BF16, "sq")
    strot = Rot(k, 2, (512,), F32, "st")
    orot = Rot(k, 3, (512,), BF16, "o")
    ob = k.buf("out", disjoint=True)
    bb = k.buf("bank")
    tiles = [(i * 512, 512, False) for i in range(4)] + [(2048, 64, True)]
    for (t0, tn, isctx) in tiles:
        A, bA, B, bB = (Ac, bAc, Bc, bBc) if isctx else (Al, bAl, Bl, bBl)
        norm_mod_dram(k, xT, t0, tn, lambda c: A[:, c:c + 1], lambda c: B[:, c:c + 1], [bA, bB],
                      [(hT, t0, BF16, orot)], cst, xrot, sqrot, strot, orot, k.banks[0], bb, [], [ob])
    return k.finish(["out"])


ADA_COLS = 6 * D // NCORES


def build_ada():
    k = K()
    cT = k.dram_in("cT", [128, NCH, 3])
    w = k.dram_in("w", [DEPTH, D, ADA_COLS])
    b = k.dram_in("b", [DEPTH, ADA_COLS])
    out = k.dram_out("mods", [DEPTH, 3, ADA_COLS])
    c_sb = k.alloc((NCH, 3), F32)
    bc = k.buf("c")
    k.dma("sp", c_sb, cT, w=[bc])
    sg = k.alloc((NCH, 3), F32)
    bsg = k.buf("sg")
    k.act(sg, c_sb, AF.Sigmoid, r=[bc], w=[bsg])
    k.tt("dve", sg, sg, c_sb, ALU.mult, r=[bsg, bc], w=[bsg])
    wrot = Rot(k, 2, (NCH, 512), F32, "w")
    brot = Rot(k, 2, (512,), F32, "b")
    orot = Rot(k, 2, (512,), F32, "o")
    ob = k.buf("out", disjoint=True)
    pb = k.bufs(2, "bank")
    i = 0
    for l in range(DEPTH):
        wl = w[l].rearrange("(c p) n -> p c n", p=128)
        for nt in range(ADA_COLS // 512):
            wt, bw = wrot.next()
            for h in range(4):
                k.dma("sp", wt[:, h * 8:(h + 1) * 8, :], wl[:, h * 8:(h + 1) * 8, nt * 512:(nt + 1) * 512], w=[bw], key=bw.key)
            bt, bbt = brot.next()
            for r in range(3):
                k.dma("sp", bt[r:r + 1, :], b[l:l + 1, nt * 512:(nt + 1) * 512], w=[bbt], key=bbt.key)
            ps = k.banks[i % 2][0:3, :]
            for c in range(NCH):
                k.mm(ps, sg[:, c, :], wt[:, c, :], start=(c == 0), stop=(c == NCH - 1), r=[bsg, bw], w=[pb[i % 2]])
            ot, bo = orot.next()
            k.tt("dve", ot[0:3, :], ps, bt[0:3, :], ALU.add, r=[pb[i % 2], bbt], w=[bo])
            k.dma("pool", out[l, :, nt * 512:(nt + 1) * 512], ot[0:3, :], r=[bo], w=[ob])
            i += 1
    return k.finish(["out"])


SEG_L = 1024 + 128
SEG_C = 66
TF = 2 * SEG_L + SEG_C
SEGS = [(0, SEG_L, 64, 1024, 0, False), (SEG_L, SEG_L, 64, 1024, 1024, False), (2 * SEG_L, SEG_C, 1, 64, 2048, True)]


def tok_tiles(n, step=512):
    return [(t, min(step, n - t)) for t in range(0, n, step)]


def build_f():
    k = K()
    cst = consts(k)
    ones, b_ones, eps, b_eps = cst
    xT = k.dram_in("xT", [NCH, 128, TF])
    yAC = k.dram_in("yAC", [24, 128, TF], BF16)
    hgB = k.dram_in("hgB", [8, 128, TF])
    ssB = k.dram_in("ssB", [4, TF])
    w_out = k.dram_in("w_out", [NCH, 128, NCH, 128])
    w_up = k.dram_in("w_up", [NCH, 2, 128, NCH, 128])
    w_down = k.dram_in("w_down", [NCH, 128, NCH, 128])
    xo = k.dram_out("xT_out", [NCH, 128, TO])
    hn = k.dram_out("hT_next", [NCH, 128, TO], BF16)
    fin = k.dram_out("fin", [NCH, 128, TO])
    xmid = k.dram_tmp("xmid", [NCH, 128, TF])
    actd = k.dram_tmp("actd", [NCH, 128, 1024], BF16)
    ml, bml = load_small(k, "mod_lat", [6, NCH])
    mc, bmc = load_small(k, "mod_ctx", [6, NCH])
    mnl, bmnl = load_small(k, "modn_lat", [6, NCH])
    mnc, bmnc = load_small(k, "modn_ctx", [6, NCH])
    g2n, bg2n = load_small(k, "g2n", [NCH])
    gn, bgn = load_small(k, "gn", [NCH])
    cw, bcw = load_small(k, "convw", [NCH, 9])
    cb, bcb = load_small(k, "convb", [NCH])
    hm, bhm = load_small(k, "hm", [6])
    A2 = {False: make_AB(k, ml, bml, g2n, bg2n, 4, 3), True: make_AB(k, mc, bmc, g2n, bg2n, 4, 3)}
    A3 = {False: make_AB(k, mnl, bmnl, gn, bgn, 1, 0), True: make_AB(k, mnc, bmnc, gn, bgn, 1, 0)}
    mods = {False: (ml, bml), True: (mc, bmc)}
    ones4 = k.alloc((128,), F32)
    b_ones4 = k.buf("ones4")
    k.memset("pool", ones4, 1.0, w=[b_ones4])

    YH = k.alloc((NCH, SEG_L), BF16)
    bYH = k.bufs(NCH, "yh")
    wrot = Rot(k, 3, (NCH, 128), BF16, "w")
    xrot = Rot(k, 4, (512,), F32, "x")
    sqrot = Rot(k, 2, (512,), BF16, "sq")
    strot = Rot(k, 1, (SEG_L,), F32, "st")
    orot = Rot(k, 2, (512,), BF16, "o")
    orot32 = Rot(k, 2, (512,), F32, "o32")
    gtrot = Rot(k, 2, (SEG_L,), F32, "gt")
    cvrot = Rot(k, 2, (1024,), F32, "cv")
    sgrot = Rot(k, 2, (1024,), F32, "sg")
    acrot = Rot(k, 2, (1024,), BF16, "ac")
    ss_sb = k.alloc((SEG_L,), F32)
    b_ss = k.buf("ss")
    rsB = k.alloc((SEG_L,), F32)
    b_rsB = k.buf("rsB")
    b_xmid = k.buf("xmid", disjoint=True)
    b_actd = k.buf("actd", disjoint=True)
    b_xo = k.buf("xo", disjoint=True, key="out")
    b_out = k.buf("out", disjoint=True)
    pbank = k.bufs(8, "bank")
    mmi = [0]

    def next_bank():
        i = 3 + (mmi[0] % 4)
        mmi[0] += 1
        return k.banks[i], pbank[i]

    for (s0, sl, oo, on, to0, isctx) in SEGS:
        tts = tok_tiles(sl)
        mod, bmod = mods[isctx]
        for c in range(24):
            cc = c if c < 12 else c + 8
            k.dma("sp", YH[:, cc, 0:sl], yAC[c, :, s0:s0 + sl], w=[bYH[cc]])
        k.dma("sp", ss_sb[0:4, 0:sl], ssB[:, s0:s0 + sl], w=[b_ss])
        for ti, (t0, tn) in enumerate(tts):
            k.mm(k.banks[ti][:, 0:tn], ones4[0:4, :], ss_sb[0:4, t0:t0 + tn], r=[b_ss, b_ones4], w=[pbank[ti]])
            k.act(rsB[:, t0:t0 + tn], k.banks[ti][:, 0:tn], AF.Sqrt, bias=eps, scale=1.0 / 1024, r=[pbank[ti], b_eps], w=[b_rsB])
        k.recip(rsB[:, 0:sl], rsB[:, 0:sl], r=[b_rsB], w=[b_rsB])
        for c in range(8):
            for (t0, tn) in tts:
                xt, bx = xrot.next()
                k.dma("sp", xt[:, 0:tn], hgB[c, :, s0 + t0:s0 + t0 + tn], w=[bx])
                k.tt("pool", YH[:, 12 + c, t0:t0 + tn], xt[:, 0:tn], rsB[:, t0:t0 + tn], ALU.mult, r=[bx, b_rsB], w=[bYH[12 + c]])
        for oc in range(NCH):
            wt, bw = wrot.next()
            k.dma("pool", wt, w_out[oc], w=[bw])
            for ti, (t0, tn) in enumerate(tts):
                bank, bb = next_bank()
                for kc in range(NCH):
                    k.mm(bank[:, 0:tn], wt[:, kc, :], YH[:, kc, t0:t0 + tn], start=(kc == 0), stop=(kc == NCH - 1),
                         r=[bw, bYH[kc]], w=[bb])
                xt, bx = xrot.next()
                k.dma("sp", xt[:, 0:tn], xT[oc, :, s0 + t0:s0 + t0 + tn], w=[bx])
                k.stt("dve", xt[:, 0:tn], bank[:, 0:tn], mod[:, 2, oc:oc + 1], xt[:, 0:tn], ALU.mult, ALU.add,
                      r=[bb, bx, bmod], w=[bx])
                k.dma("pool", xmid[oc, :, s0 + t0:s0 + t0 + tn], xt[:, 0:tn], r=[bx], w=[b_xmid])
                sq, bs = sqrot.next()
                k.act(sq[:, 0:tn], xt[:, 0:tn], AF.Square, r=[bx], w=[bs])
                k.mm(k.banks[ti][:, 0:tn], ones, sq[:, 0:tn], start=(oc == 0), stop=(oc == NCH - 1), r=[bs, b_ones], w=[pbank[ti]])
        rs, brs = strot.next()
        for ti, (t0, tn) in enumerate(tts):
            k.act(rs[:, t0:t0 + tn], k.banks[ti][:, 0:tn], AF.Sqrt, bias=eps, scale=1.0 / D, r=[pbank[ti], b_eps], w=[brs])
        k.recip(rs[:, 0:sl], rs[:, 0:sl], r=[brs], w=[brs])
        A, bA, B, bB = A2[isctx]
        for c in range(NCH):
            for (t0, tn) in tts:
                xt, bx = xrot.next()
                k.dma("sp", xt[:, 0:tn], xmid[c, :, s0 + t0:s0 + t0 + tn], r=[b_xmid], w=[bx])
                k.stt("dve", xt[:, 0:tn], xt[:, 0:tn], A[:, c:c + 1], rs[:, t0:t0 + tn], ALU.mult, ALU.mult, r=[bx, brs, bA], w=[bx])
                k.act(YH[:, c, t0:t0 + tn], xt[:, 0:tn], AF.Identity, bias=B[:, c:c + 1], scale=1.0, r=[bx, bB], w=[bYH[c]])
        for fc in range(NCH):
            wu, bwu = wrot.next()
            k.dma("pool", wu, w_up[fc, 0], w=[bwu])
            wg, bwg = wrot.next()
            k.dma("pool", wg, w_up[fc, 1], w=[bwg])
            gt, bgt = gtrot.next()
            for (t0, tn) in tts:
                bank, bb = next_bank()
                for kc in range(NCH):
                    k.mm(bank[:, 0:tn], wg[:, kc, :], YH[:, kc, t0:t0 + tn], start=(kc == 0), stop=(kc == NCH - 1),
                         r=[bwg, bYH[kc]], w=[bb])
                k.cp("act", gt[:, t0:t0 + tn], bank[:, 0:tn], r=[bb], w=[bgt])
            hi = 4 if isctx else (0 if s0 == 0 else 2)
            k.ts("pool", gt[:, 0:oo], gt[:, 0:oo], hm[:, hi:hi + 1], ALU.mult, r=[bgt, bhm], w=[bgt])
            k.ts("pool", gt[:, oo + on:sl], gt[:, oo + on:sl], hm[:, hi + 1:hi + 2], ALU.mult, r=[bgt, bhm], w=[bgt])
            cv, bcv = cvrot.next()
            wcol = lambda i: cw[:, fc, i:i + 1]
            if isctx:
                k.ts("dve", cv[:, 0:on], gt[:, 1:1 + on], wcol(4), ALU.mult, cb[:, fc:fc + 1], ALU.add, r=[bgt, bcw, bcb], w=[bcv])
                k.stt("dve", cv[:, 0:on], gt[:, 0:on], wcol(3), cv[:, 0:on], ALU.mult, ALU.add, r=[bgt, bcw, bcv], w=[bcv])
                k.stt("dve", cv[:, 0:on], gt[:, 2:2 + on], wcol(5), cv[:, 0:on], ALU.mult, ALU.add, r=[bgt, bcw, bcv], w=[bcv])
            else:
                R = on // 64
                g3 = gt[:, 0:sl].rearrange("p (r c) -> p r c", c=64)
                c3 = cv[:, 0:on].rearrange("p (r c) -> p r c", c=64)
                k.ts("dve", cv[:, 0:on], gt[:, oo:oo + on], wcol(4), ALU.mult, cb[:, fc:fc + 1], ALU.add, r=[bgt, bcw, bcb], w=[bcv])
                n = 0
                for dy in (-1, 0, 1):
                    for dx in (-1, 0, 1):
                        if dy == 0 and dx == 0:
                            continue
                        wi = (dy + 1) * 3 + (dx + 1)
                        oc0, oc1 = (1, 64) if dx == -1 else ((0, 63) if dx == 1 else (0, 64))
                        ic0, ic1 = oc0 + dx, oc1 + dx
                        eng = "dve"
                        n += 1
                        k.stt(eng, c3[:, :, oc0:oc1], g3[:, 1 + dy:1 + dy + R, ic0:ic1], wcol(wi), c3[:, :, oc0:oc1],
                              ALU.mult, ALU.add, r=[bgt, bcw, bcv], w=[bcv])
            sg, bsg = sgrot.next()
            k.act(sg[:, 0:on], cv[:, 0:on], AF.Sigmoid, r=[bcv], w=[bsg])
            k.tt("pool", sg[:, 0:on], sg[:, 0:on], cv[:, 0:on], ALU.mult, r=[bsg, bcv], w=[bsg])
            ac, bac = acrot.next()
            for (t0, tn) in tok_tiles(on):
                bank, bb = next_bank()
                for kc in range(NCH):
                    k.mm(bank[:, 0:tn], wu[:, kc, :], YH[:, kc, oo + t0:oo + t0 + tn], start=(kc == 0), stop=(kc == NCH - 1),
                         r=[bwu, bYH[kc]], w=[bb])
                k.tt("dve", ac[:, t0:t0 + tn], bank[:, 0:tn], sg[:, t0:t0 + tn], ALU.mult, r=[bb, bsg], w=[bac])
            k.dma("sp", actd[fc, :, 0:on], ac[:, 0:on], r=[bac], w=[b_actd])
        for c in range(NCH):
            k.dma("sp", YH[:, c, 0:on], actd[c, :, 0:on], r=[b_actd], w=[bYH[c]])
        otts = tok_tiles(on)
        for oc in range(NCH):
            wt, bw = wrot.next()
            k.dma("pool", wt, w_down[oc], w=[bw])
            for ti, (t0, tn) in enumerate(otts):
                bank, bb = next_bank()
                for kc in range(NCH):
                    k.mm(bank[:, 0:tn], wt[:, kc, :], YH[:, kc, t0:t0 + tn], start=(kc == 0), stop=(kc == NCH - 1),
                         r=[bw, bYH[kc]], w=[bb])
                xt, bx = xrot.next()
                k.dma("sp", xt[:, 0:tn], xmid[oc, :, s0 + oo + t0:s0 + oo + t0 + tn], r=[b_xmid], w=[bx])
                k.stt("dve", xt[:, 0:tn], bank[:, 0:tn], mod[:, 5, oc:oc + 1], xt[:, 0:tn], ALU.mult, ALU.add,
                      r=[bb, bx, bmod], w=[bx])
                k.dma("pool", xo[oc, :, to0 + t0:to0 + t0 + tn], xt[:, 0:tn], r=[bx], w=[b_xo])
        A, bA, B, bB = A3[isctx]
        for ti, (t0, tn) in enumerate(otts):
            norm_mod_dram(k, xo, to0 + t0, tn, lambda c: A[:, c:c + 1], lambda c: B[:, c:c + 1], [bA, bB],
                          [(hn, to0 + t0, BF16, orot), (fin, to0 + t0, F32, orot32)], cst, xrot, sqrot, strot, orot,
                          k.banks[ti], pbank[ti], [b_xo], [b_out])
    return k.finish(["out"])


GLA_LVL = int(os.environ.get('GLA_LVL', '9'))
M_TILES = [(0, 256)] + [(256 + 512 * i, 512) for i in range(16)]
CH = 64


def build_m(tiles=None):
    k = K()
    cst = consts(k)
    ones, b_ones, eps, b_eps = cst
    MT = tiles or M_TILES
    T = MT[-1][0] + MT[-1][1]
    hT = k.dram_in("hT", [NCH, 128, T], BF16)
    wA = k.dram_in("wA", [3, 128, NCH, 512])
    wB = k.dram_in("wB", [128, NCH, 512])
    wC1 = k.dram_in("wC1", [128, NCH, 416])
    wCg = k.dram_in("wCg", [128, NCH, 384])
    wV = k.dram_in("wV", [128, NCH, 768])
    yA = k.dram_out("yA", [3, 128, T], BF16)
    yC = k.dram_out("yC", [3, 128, T], BF16)
    hgB = k.dram_out("hgB", [2, 128, T])
    ssB = k.dram_out("ssB", [1, T])
    sA = k.dram_tmp("sA", [3, 6, 128, T])
    sB = k.dram_tmp("sB", [4, 128, T])
    sC = k.dram_tmp("sC", [11, 128, T])
    vtm = k.dram_tmp("vtm", [T, 768], BF16)
    oF = k.dram_tmp("oF", [8, 128, T])
    b_scr = k.buf("scr", disjoint=True)
    b_oF = k.buf("oF", disjoint=True)
    b_out = k.buf("out", disjoint=True)
    pbank = k.bufs(8, "bank")

    lgf, blgf = load_small(k, "lbf", [3, 4])
    lgb, blgb = load_small(k, "lbb", [3, 4])
    lmask, blmask = load_small(k, "lmask", [4])
    gA, bgA = load_small(k, "gA", [3])
    gC, bgC = load_small(k, "gC", [3])
    gB, bgB = load_small(k, "gB", [2])
    cwB, bcwB = load_small(k, "convwB", [2, 4])
    cbB, bcbB = load_small(k, "convbB", [2])
    brs = {}
    for nm in ("br_f", "bi_f", "br_b", "bi_b", "lam_f", "lam_b", "ba_f", "ba_b"):
        brs[nm] = load_small(k, nm, [2])
    wgate = {}
    for nm in ("wr_f", "wi_f", "wr_b", "wi_b"):
        d = k.dram_in(nm, [128, 2, 128])
        v = k.alloc((2, 128), BF16)
        b = k.buf(nm)
        k.dma("pool", v, d, w=[b])
        wgate[nm] = (v, b)
    wa = {}
    for nm in ("wa_f", "wa_b"):
        d = k.dram_in(nm, [16, 192])
        v = k.alloc((192,), BF16)
        b = k.buf(nm)
        k.dma("pool", v[0:16, :], d, w=[b])
        wa[nm] = (v, b)
    ident = k.alloc((128,), BF16)
    b_id = k.buf("ident")
    k.memset("pool", ident, 1.0, w=[b_id])
    k.P.op("pool", lambda e: e.affine_select(out=ident, in_=ident, pattern=[[-1, 128]], compare_op=ALU.is_equal, fill=0.0,
                                             base=0, channel_multiplier=1), reads=[b_id], writes=[b_id])
    mk_f = k.alloc((CH,), BF16)
    mk_b = k.alloc((CH,), BF16)
    b_mk = k.buf("mk")
    k.memset("pool", mk_f, 1.0, w=[b_mk])
    k.memset("pool", mk_b, 1.0, w=[b_mk])
    k.P.op("pool", lambda e: e.affine_select(out=mk_f[0:CH, :], in_=mk_f[0:CH, :], pattern=[[1, CH]], compare_op=ALU.is_ge, fill=0.0,
                                             base=0, channel_multiplier=-1), reads=[b_mk], writes=[b_mk])
    k.P.op("pool", lambda e: e.affine_select(out=mk_b[0:CH, :], in_=mk_b[0:CH, :], pattern=[[-1, CH]], compare_op=ALU.is_ge, fill=0.0,
                                             base=0, channel_multiplier=1), reads=[b_mk], writes=[b_mk])
    rm_f = k.alloc((8, CH), F32)
    rm_b = k.alloc((8, CH), F32)
    b_rm = k.buf("rm")
    k.memset("pool", rm_f, 1.0, w=[b_rm])
    k.memset("pool", rm_b, 1.0, w=[b_rm])
    k.memset("pool", rm_f[:, :, 0:1], 0.0, w=[b_rm])
    k.memset("pool", rm_b[:, :, CH - 1:CH], 0.0, w=[b_rm])
    rm_f2 = rm_f.rearrange("p a b -> p (a b)")
    rm_b2 = rm_b.rearrange("p a b -> p (a b)")

    def lower_bound(lg, blg, nm):
        e = k.alloc((3, 4), F32)
        be = k.buf(nm + "e")
        k.act(e, lg, AF.Exp, r=[blg], w=[be])
        s = k.alloc((3,), F32)
        num = k.alloc((3,), F32)
        bs = k.buf(nm + "s")
        k.tt("dve", s, e[:, :, 0], e[:, :, 1], ALU.add, r=[be], w=[bs])
        k.tt("dve", s, s, e[:, :, 2], ALU.add, r=[be, bs], w=[bs])
        k.tt("dve", s, s, e[:, :, 3], ALU.add, r=[be, bs], w=[bs])
        k.ts("dve", num, e[:, :, 0], lmask[:, 0:1], ALU.mult, r=[be, blmask], w=[bs])
        for l in range(1, 4):
            k.stt("dve", num, e[:, :, l], lmask[:, l:l + 1], num, ALU.mult, ALU.add, r=[be, blmask, bs], w=[bs])
        k.recip(s, s, r=[bs], w=[bs])
        lb = k.alloc((3,), F32)
        oml = k.alloc((3,), F32)
        noml = k.alloc((3,), F32)
        k.tt("dve", lb, num, s, ALU.mult, r=[bs], w=[bs])
        k.ts("dve", lb, lb, 0.0, ALU.max, 1.0, ALU.min, r=[bs], w=[bs])
        k.ts("dve", oml, lb, -1.0, ALU.mult, 1.0, ALU.add, r=[bs], w=[bs])
        k.ts("dve", noml, oml, -1.0, ALU.mult, r=[bs], w=[bs])
        return lb, oml, noml, bs

    LBf = lower_bound(lgf, blgf, "lf")
    LBb = lower_bound(lgb, blgb, "lb")
    CL = {}
    for d_ in ("f", "b"):
        lam, blam = brs["lam_" + d_]
        cl = k.alloc((2,), F32)
        cl2 = k.alloc((2,), F32)
        bcl = k.buf("cl" + d_)
        k.act(cl, lam, AF.Exp, scale=-1.0, r=[blam], w=[bcl])
        k.ts("dve", cl, cl, 1.0, ALU.add, r=[bcl], w=[bcl])
        k.act(cl, cl, AF.Ln, r=[bcl], w=[bcl])
        k.ts("dve", cl2, cl, -16.0, ALU.mult, r=[bcl], w=[bcl])
        k.ts("dve", cl, cl, -8.0, ALU.mult, r=[bcl], w=[bcl])
        CL[d_] = (cl, cl2, bcl)
    mark = k.off

    wrot = Rot(k, 2, (NCH, 768), BF16, "w")
    hrot = Rot(k, 2, (NCH, 512), BF16, "h")
    srot = Rot(k, 4, (512,), F32, "s")
    s2rot = Rot(k, 3, (512,), F32, "s2")
    vrot = Rot(k, 2, (768,), BF16, "v")
    abrot = Rot(k, 2, (512,), BF16, "ab")
    bi = [0]

    def nb():
        i = bi[0] % 6
        bi[0] += 1
        return k.banks[i], pbank[i]

    def store(dst, src, bsrc):
        k.dma("sp", dst, src, r=[bsrc], w=[b_scr])

    def ev_raw(bank, bb, M, tn, dst):
        st_, bs = srot.next()
        k.cp("act", st_[0:M, 0:tn], bank[0:M, 0:tn], r=[bb], w=[bs])
        store(dst, st_[0:M, 0:tn], bs)

    def ev_silu(bank, bb, M, tn, dst):
        st_, bs = srot.next()
        k.act(st_[0:M, 0:tn], bank[0:M, 0:tn], AF.Sigmoid, r=[bb], w=[bs])
        k.tt("dve", st_[0:M, 0:tn], st_[0:M, 0:tn], bank[0:M, 0:tn], ALU.mult, r=[bs, bb], w=[bs])
        store(dst, st_[0:M, 0:tn], bs)

    def ev_gelu(bank, bb, M, tn, dst):
        st_, bs = srot.next()
        s2, b2 = s2rot.next()
        k.act(st_[0:M, 0:tn], bank[0:M, 0:tn], AF.Square, r=[bb], w=[bs])
        k.ts("dve", st_[0:M, 0:tn], st_[0:M, 0:tn], 0.044715, ALU.mult, 1.0, ALU.add, r=[bs], w=[bs])
        k.tt("dve", st_[0:M, 0:tn], st_[0:M, 0:tn], bank[0:M, 0:tn], ALU.mult, r=[bs, bb], w=[bs])
        k.act(s2[0:M, 0:tn], st_[0:M, 0:tn], AF.Sigmoid, scale=1.5957691216057308, r=[bs], w=[b2])
        k.tt("dve", s2[0:M, 0:tn], s2[0:M, 0:tn], bank[0:M, 0:tn], ALU.mult, r=[b2, bb], w=[b2])
        store(dst, s2[0:M, 0:tn], b2)

    def ev_z(LB, h):
        lb, oml, noml, blb = LB

        def f(bank, bb, M, tn, dst):
            dstf, dstk = dst
            st_, bs = srot.next()
            s2, b2 = s2rot.next()
            s3, b3 = s2rot.next()
            k.act(st_[:, 0:tn], bank[:, 0:tn], AF.Sigmoid, r=[bb], w=[bs])
            k.ts("dve", s2[:, 0:tn], st_[:, 0:tn], oml[:, h:h + 1], ALU.mult, lb[:, h:h + 1], ALU.add, r=[bs, blb], w=[b2])
            k.ts("pool", s3[:, 0:tn], st_[:, 0:tn], noml[:, h:h + 1], ALU.mult, oml[:, h:h + 1], ALU.add, r=[bs, blb], w=[b3])
            store(dstf, s2[:, 0:tn], b2)
            store(dstk, s3[:, 0:tn], b3)
        return f

    def ev_lowrank(d_):
        wv, bwv = wa["wa_" + d_]
        ba, bba = brs["ba_" + d_]

        def f(bank, bb, M, tn, dst):
            ab, bab = abrot.next()
            k.cp("act", ab[0:16, 0:tn], bank[0:16, 0:tn], r=[bb], w=[bab])
            for kt, (ko, ks) in enumerate(((0, 128), (128, 64))):
                bank2, bb2 = nb()
                k.mm(bank2[0:ks, 0:tn], wv[0:16, ko:ko + ks], ab[0:16, 0:tn], r=[bwv, bab], w=[bb2])
                st_, bs = srot.next()
                k.act(st_[0:ks, 0:tn], bank2[0:ks, 0:tn], AF.Sigmoid, bias=ba[0:ks, kt:kt + 1], r=[bb2, bba], w=[bs])
                store(dst[kt], st_[0:ks, 0:tn], bs)
        return f

    def run_group(wsrc, ncols, fm_tiles, tm=None):
        wt, bw = wrot.next()
        k.dma("pool", wt[:, :, 0:ncols], wsrc, w=[bw])
        for (t0, tn) in MT:
            ht, bh = hrot.next()
            for g_ in range(8):
                k.dma("sp", ht[:, 4 * g_:4 * g_ + 4, 0:tn], hT[4 * g_:4 * g_ + 4, :, t0:t0 + tn].rearrange("c p t -> p c t"), w=[bh], key=bh.key)
            for (c0, M, evac, dstf) in fm_tiles:
                bank, bb = nb()
                for kc in range(NCH):
                    k.mm(bank[0:M, 0:tn], wt[:, kc, c0:c0 + M], ht[:, kc, 0:tn], start=(kc == 0), stop=(kc == NCH - 1),
                         r=[bw, bh], w=[bb])
                evac(bank, bb, M, tn, dstf(t0, tn))
            if tm is not None:
                for s_ in range(tn // 128):
                    vt, bv = vrot.next()
                    for half in range(2):
                        bank, bb = nb()
                        for kc in range(NCH):
                            k.mm(bank[:, 0:384], ht[:, kc, s_ * 128:(s_ + 1) * 128], wt[:, kc, half * 384:(half + 1) * 384],
                                 start=(kc == 0), stop=(kc == NCH - 1), r=[bw, bh], w=[bb])
                        k.cp("act" if half == 0 else "dve", vt[:, half * 384:(half + 1) * 384], bank[:, 0:384], r=[bb], w=[bv])
                    k.dma("sp", vtm[t0 + s_ * 128:t0 + (s_ + 1) * 128, :], vt, r=[bv], w=[b_scr])

    sl = lambda arr, i: (lambda t0, tn: arr[i, :, t0:t0 + tn])
    for h in range(3):
        run_group(wA[h], 512, [
            (0, 128, ev_silu, (lambda h: lambda t0, tn: sA[h, 0, :, t0:t0 + tn])(h)),
            (128, 128, ev_z(LBf, h), (lambda h: lambda t0, tn: (sA[h, 1, :, t0:t0 + tn], sA[h, 3, :, t0:t0 + tn]))(h)),
            (256, 128, ev_z(LBb, h), (lambda h: lambda t0, tn: (sA[h, 2, :, t0:t0 + tn], sA[h, 4, :, t0:t0 + tn]))(h)),
            (384, 128, ev_silu, (lambda h: lambda t0, tn: sA[h, 5, :, t0:t0 + tn])(h)),
        ])
    run_group(wB, 512, [(0, 128, ev_raw, sl(sB, 0)), (128, 128, ev_raw, sl(sB, 1)),
                        (256, 128, ev_gelu, sl(sB, 2)), (384, 128, ev_gelu, sl(sB, 3))])
    run_group(wC1, 416, [
        (0, 128, ev_raw, sl(sC, 0)), (128, 64, ev_raw, lambda t0, tn: sC[1, 0:64, t0:t0 + tn]),
        (192, 128, ev_raw, sl(sC, 2)), (320, 64, ev_raw, lambda t0, tn: sC[3, 0:64, t0:t0 + tn]),
        (384, 16, ev_lowrank("f"), lambda t0, tn: (sC[4, :, t0:t0 + tn], sC[5, 0:64, t0:t0 + tn])),
        (400, 16, ev_lowrank("b"), lambda t0, tn: (sC[6, :, t0:t0 + tn], sC[7, 0:64, t0:t0 + tn])),
    ])
    run_group(wCg, 384, [(0, 128, ev_silu, sl(sC, 8)), (128, 128, ev_silu, sl(sC, 9)), (256, 128, ev_silu, sl(sC, 10))])
    run_group(wV, 768, [], tm=(0, 768))
    k.P.barrier()
    k.off = mark
    if os.environ.get('M_P2_ONLY'):
        k.dma('sp', ssB[0:1, 0:512], sB[0, 0:1, 0:512], r=[b_scr], w=[b_out])
        return k.finish(['out'])

    SC = 512
    frot = Rot(k, 4, (SC,), F32, "f")
    bbrot = Rot(k, 4, (SC,), F32, "bb")
    curot = Rot(k, 4, (SC,), F32, "cu")
    erot = Rot(k, 6, (SC,), F32, "e")
    qkrot = Rot(k, 6, (SC,), F32, "qk")
    qtrot = Rot(k, 4, (SC,), BF16, "qt")
    ktrot = Rot(k, 4, (SC,), BF16, "kt")
    smrot = Rot(k, 8, (8, 4), F32, "sm")
    vrot3 = Rot(k, 2, (8, 384), BF16, "v3")
    ktmrot = Rot(k, 2, (8, 192), BF16, "ktm")
    attrot = Rot(k, 3, (CH,), BF16, "att")
    srrot = Rot(k, 4, (384,), BF16, "sr")
    tmprot = Rot(k, 3, (384,), F32, "tmp")
    orot = Rot(k, 2, (3, SC), F32, "oacc")
    o2rot = Rot(k, 2, (3, SC), F32, "o2")
    sqrot = Rot(k, 3, (SC,), BF16, "sq")
    rsrot = Rot(k, 2, (SC,), F32, "rs")
    yrot = Rot(k, 3, (SC,), BF16, "y")
    S_sb = k.alloc((2, 384), F32)
    bS = k.buf("S")
    pi = [0]

    def nb3():
        i = pi[0] % 7
        pi[0] += 1
        return k.banks[i], pbank[i]

    pt_bf = k.bank_bf[:, :]
    b_pt = k.bufs(4, "pt")
    ti_ = [0]

    def gla_dir(qsrc, ksrc, fsrc, vcol0, V, gs, qscale, d_, oidx, final=None):
        nk = len(qsrc)
        nv = V // 128
        fwd = d_ == "f"
        k.memset("pool", S_sb, 0.0, w=[bS])
        rm2 = rm_f2 if fwd else rm_b2
        mk = mk_f if fwd else mk_b
        tiles = MT if fwd else ([MT[0]] + MT[:0:-1])
        for (t0, tn) in tiles:
            nch = tn // CH
            qts, kts_, sms = [], [], []
            for kt in range(nk):
                fap, ks = fsrc[kt]
                ft, bf = frot.next()
                k.dma("sp", ft[0:ks, 0:tn], fap[:, t0:t0 + tn], r=[b_scr], w=[bf])
                k.act(ft[0:ks, 0:tn], ft[0:ks, 0:tn], AF.Ln, r=[bf], w=[bf])
                bt, bbt = bbrot.next()
                cu, bcu = curot.next()
                if fwd:
                    k.scan(cu[0:ks, 0:tn], rm2[0:ks, 0:tn], ft[0:ks, 0:tn], 0.0, r=[bf, b_rm], w=[bcu])
                else:
                    k.scan(cu[0:ks, 0:tn][:, ::-1], rm2[0:ks, 0:tn][:, ::-1], ft[0:ks, 0:tn][:, ::-1], 0.0, r=[bf, b_rm], w=[bcu])
                bf = bcu
                f3 = cu[0:ks, 0:tn].rearrange("p (c j) -> p c j", j=CH)
                b3 = bt[0:ks, 0:tn].rearrange("p (c j) -> p c j", j=CH)
                mid = CH // 2 - 1 if fwd else CH // 2
                last = CH - 1 if fwd else 0
                k.tt("dve", b3, f3, f3[:, :, mid:mid + 1].to_broadcast([ks, nch, CH]), ALU.subtract, r=[bf], w=[bbt])
                sm, bsm = smrot.next()
                k.act(sm[0:ks, 0:nch, 0:1], f3[:, :, mid:mid + 1], AF.Exp, scale=gs, r=[bf], w=[bsm])
                k.act(sm[0:ks, 0:nch, 1:2], f3[:, :, last:last + 1], AF.Exp, scale=gs, r=[bf], w=[bsm])
                k.act(sm[0:ks, 0:nch, 2:3], b3[:, :, last:last + 1], AF.Exp, scale=gs, r=[bbt], w=[bsm])
                ep, bep = erot.next()
                em, bem = erot.next()
                k.act(ep[0:ks, 0:tn], bt[0:ks, 0:tn], AF.Exp, scale=gs, r=[bbt], w=[bep])
                k.act(em[0:ks, 0:tn], bt[0:ks, 0:tn], AF.Exp, scale=-gs, r=[bbt], w=[bem])
                qap, _ = qsrc[kt]
                kap, _ = ksrc[kt]
                qf, bqf = qkrot.next()
                kf_, bkf = qkrot.next()
                k.dma("sp", qf[0:ks, 0:tn], qap[:, t0:t0 + tn], r=[b_scr], w=[bqf])
                k.dma("sp", kf_[0:ks, 0:tn], kap[:, t0:t0 + tn], r=[b_scr], w=[bkf])
                qt, bqt = qtrot.next()
                kt_, bkt = ktrot.next()
                k.stt("dve", qt[0:ks, 0:tn], ep[0:ks, 0:tn], qscale, qf[0:ks, 0:tn], ALU.mult, ALU.mult, r=[bep, bqf], w=[bqt])
                k.tt("pool", kt_[0:ks, 0:tn], em[0:ks, 0:tn], kf_[0:ks, 0:tn], ALU.mult, r=[bem, bkf], w=[bkt])
                qts.append((qt, bqt, ks))
                kts_.append((kt_, bkt, ks))
                sms.append((sm, bsm))
            if GLA_LVL < 2:
                continue
            ktm, bktm = ktmrot.next()
            for c in range(nch):
                for kt in range(nk):
                    kt_, bkt, ks = kts_[kt]
                    ptb, bptb = nb3()
                    k.mm(ptb[0:CH, 0:ks], kt_[0:ks, c * CH:(c + 1) * CH], ident[0:ks, 0:ks], r=[bkt, b_id], w=[bptb])
                    k.cp("act", ktm[0:CH, c, kt * 128:kt * 128 + ks], ptb[0:CH, 0:ks], r=[bptb], w=[bktm])
            if GLA_LVL < 3:
                continue
            v3, bv3 = vrot3.next()
            k.dma("sp", v3[0:CH, 0:nch, 0:V], vtm[t0:t0 + tn, vcol0:vcol0 + V].rearrange("(c p) v -> p c v", p=CH), r=[b_scr], w=[bv3])
            oa, boa = orot.next()
            order = range(nch) if fwd else range(nch - 1, -1, -1)
            for c in order:
                cs = slice(c * CH, (c + 1) * CH)
                pa, bpa = nb3()
                for kt in range(nk):
                    qt, bqt, ks = qts[kt]
                    kt_, bkt, _ = kts_[kt]
                    k.mm(pa[0:CH, 0:CH], kt_[0:ks, cs], qt[0:ks, cs], start=(kt == 0), stop=(kt == nk - 1), r=[bkt, bqt], w=[bpa])
                at, bat = attrot.next()
                k.tt("dve", at[0:CH, :], pa[0:CH, 0:CH], mk[0:CH, :], ALU.mult, r=[bpa, b_mk], w=[bat])
                if GLA_LVL < 4:
                    continue
                srs = []
                for kt in range(nk):
                    _, _, ks = qts[kt]
                    sm, bsm = sms[kt]
                    sr, bsr = srrot.next()
                    k.ts("pool", sr[0:ks, 0:V], S_sb[0:ks, kt, 0:V], sm[0:ks, c, 0:1], ALU.mult, r=[bS, bsm], w=[bsr])
                    srs.append((sr, bsr))
                for vt in range(nv):
                    po, bpo = nb3()
                    k.mm(po[:, 0:CH], v3[0:CH, c, vt * 128:(vt + 1) * 128], at[0:CH, :], start=True, stop=False, r=[bv3, bat], w=[bpo])
                    for kt in range(nk):
                        qt, bqt, ks = qts[kt]
                        sr, bsr = srs[kt]
                        k.mm(po[:, 0:CH], sr[0:ks, vt * 128:(vt + 1) * 128], qt[0:ks, cs], start=False, stop=(kt == nk - 1),
                             r=[bsr, bqt], w=[bpo])
                    k.cp("act", oa[:, vt, cs], po[:, 0:CH], r=[bpo], w=[boa])
                if GLA_LVL < 5:
                    continue
                for kt in range(nk):
                    _, _, ks = qts[kt]
                    sm, bsm = sms[kt]
                    pm, bpm = nb3()
                    k.mm(pm[0:ks, 0:V], ktm[0:CH, c, kt * 128:kt * 128 + ks], v3[0:CH, c, 0:V], r=[bktm, bv3], w=[bpm])
                    tp, btp = tmprot.next()
                    k.ts("dve", tp[0:ks, 0:V], pm[0:ks, 0:V], sm[0:ks, c, 2:3], ALU.mult, r=[bpm, bsm], w=[btp])
                    k.stt("dve", S_sb[0:ks, kt, 0:V], S_sb[0:ks, kt, 0:V], sm[0:ks, c, 1:2], tp[0:ks, 0:V], ALU.mult, ALU.add,
                          r=[bS, bsm, btp], w=[bS])
            if GLA_LVL < 6:
                continue
            if fwd:
                for vt in range(nv):
                    k.dma("pool", oF[oidx + vt, :, t0:t0 + tn], oa[:, vt, 0:tn], r=[boa], w=[b_oF])
            else:
                gsrcs, gain, bgain, ydst = final
                o2, bo2 = o2rot.next()
                for vt in range(nv):
                    k.dma("sp", o2[:, vt, 0:tn], oF[oidx + vt, :, t0:t0 + tn], r=[b_oF], w=[bo2])
                pst, bpst = nb3()
                for vt in range(nv):
                    k.tt("dve", oa[:, vt, 0:tn], oa[:, vt, 0:tn], o2[:, vt, 0:tn], ALU.add, r=[boa, bo2], w=[boa])
                    sq, bsq = sqrot.next()
                    k.act(sq[:, 0:tn], oa[:, vt, 0:tn], AF.Square, r=[boa], w=[bsq])
                    k.mm(pst[:, 0:tn], ones, sq[:, 0:tn], start=(vt == 0), stop=(vt == nv - 1), r=[bsq, b_ones], w=[bpst])
                rs, brs_ = rsrot.next()
                k.act(rs[:, 0:tn], pst[:, 0:tn], AF.Sqrt, bias=eps, scale=1.0 / V, r=[bpst, b_eps], w=[brs_])
                k.recip(rs[:, 0:tn], rs[:, 0:tn], r=[brs_], w=[brs_])
                for vt in range(nv):
                    k.dma("sp", o2[:, vt, 0:tn], gsrcs[vt][:, t0:t0 + tn], r=[b_scr, bo2], w=[bo2])
                    k.stt("dve", oa[:, vt, 0:tn], oa[:, vt, 0:tn], gain[:, vt:vt + 1], rs[:, 0:tn], ALU.mult, ALU.mult,
                          r=[boa, bgain, brs_], w=[boa])
                    yt, byt = yrot.next()
                    k.tt("pool", yt[:, 0:tn], oa[:, vt, 0:tn], o2[:, vt, 0:tn], ALU.mult, r=[boa, bo2], w=[byt])
                    k.dma("pool", ydst[vt][:, t0:t0 + tn], yt[:, 0:tn], r=[byt], w=[b_out])

    _skip = os.environ.get('M_SKIP', '')
    for h in range(0 if 'A' in _skip else 3):
        for d_ in ("f", "b"):
            fi, ki = (1, 3) if d_ == "f" else (2, 4)
            gla_dir([(sA[h, 0], 128)], [(sA[h, ki], 128)], [(sA[h, fi], 128)], h * 128, 128, 1.0, 128.0 ** -0.5, d_, h,
                    final=([sA[h, 5]], gA[:, h:h + 1], bgA, [yA[h]]))
    for d_ in (() if 'C' in _skip else ("f", "b")):
        fo = 4 if d_ == "f" else 6
        gla_dir([(sC[0], 128), (sC[1, 0:64], 64)], [(sC[2], 128), (sC[3, 0:64], 64)], [(sC[fo], 128), (sC[fo + 1, 0:64], 64)],
                384, 384, 1.0 / 16.0, 192.0 ** -0.5, d_, 3,
                final=([sC[8], sC[9], sC[10]], gC, bgC, [yC[0], yC[1], yC[2]]))

    xrotB = Rot(k, 3, (SC + 4,), F32, "xB")
    xcrot = Rot(k, 3, (SC,), F32, "xc")
    xcbrot = Rot(k, 3, (SC,), BF16, "xcb")
    grot = Rot(k, 6, (SC,), F32, "g")
    hrotB = Rot(k, 4, (SC,), F32, "hB")
    carry = k.alloc((2,), F32)
    b_carry = k.buf("carry")
    for d_ in (() if 'B' in _skip else ("f", "b")):
        fwd = d_ == "f"
        cl, cl2, bcl = CL[d_]
        k.memset("pool", carry, 0.0, w=[b_carry])
        tiles = MT if fwd else ([MT[0]] + MT[:0:-1])
        for (t0, tn) in tiles:
            seg0, seg1 = (0, CTX) if t0 < CTX else (CTX, T)
            pss, bpss = nb3()
            for n in range(2):
                xt, bx = xrotB.next()
                lo = max(t0 - 2, seg0)
                hi = min(t0 + tn + 1, seg1)
                if lo > t0 - 2 or hi < t0 + tn + 1:
                    k.memset("pool", xt[:, 0:tn + 3], 0.0, w=[bx])
                k.dma("sp", xt[:, lo - (t0 - 2):hi - (t0 - 2)], sB[n, :, lo:hi], r=[b_scr], w=[bx])
                xc, bxc = xcrot.next()
                k.ts("dve", xc[:, 0:tn], xt[:, 0:tn], cwB[:, n, 0:1], ALU.mult, cbB[:, n:n + 1], ALU.add, r=[bx, bcwB, bcbB], w=[bxc])
                for j in range(1, 4):
                    k.stt("dve", xc[:, 0:tn], xt[:, j:j + tn], cwB[:, n, j:j + 1], xc[:, 0:tn], ALU.mult, ALU.add,
                          r=[bx, bcwB, bxc], w=[bxc])
                xcb, bxcb = xcbrot.next()
                k.cp("pool", xcb[:, 0:tn], xc[:, 0:tn], r=[bxc], w=[bxcb])
                wr, bwr = wgate["wr_" + d_]
                wi, bwi = wgate["wi_" + d_]
                br, bbr = brs["br_" + d_]
                bi_, bbi = brs["bi_" + d_]
                pr, bpr = nb3()
                k.mm(pr[:, 0:tn], wr[:, n, :], xcb[:, 0:tn], r=[bwr, bxcb], w=[bpr])
                pi_, bpi = nb3()
                k.mm(pi_[:, 0:tn], wi[:, n, :], xcb[:, 0:tn], r=[bwi, bxcb], w=[bpi])
                rg, brg = grot.next()
                ig, big = grot.next()
                k.act(rg[:, 0:tn], pr[:, 0:tn], AF.Sigmoid, bias=br[:, n:n + 1], r=[bpr, bbr], w=[brg])
                k.act(ig[:, 0:tn], pi_[:, 0:tn], AF.Sigmoid, bias=bi_[:, n:n + 1], r=[bpi, bbi], w=[big])
                a_, ba_ = grot.next()
                a2, ba2 = grot.next()
                k.act(a_[:, 0:tn], rg[:, 0:tn], AF.Exp, scale=cl[:, n:n + 1], r=[brg, bcl], w=[ba_])
                k.act(a2[:, 0:tn], rg[:, 0:tn], AF.Exp, scale=cl2[:, n:n + 1], r=[brg, bcl], w=[ba2])
                k.ts("dve", a2[:, 0:tn], a2[:, 0:tn], -1.0, ALU.mult, 1.0, ALU.add, r=[ba2], w=[ba2])
                k.ts("dve", a2[:, 0:tn], a2[:, 0:tn], 1e-30, ALU.max, r=[ba2], w=[ba2])
                k.act(a2[:, 0:tn], a2[:, 0:tn], AF.Ln, r=[ba2], w=[ba2])
                k.act(a2[:, 0:tn], a2[:, 0:tn], AF.Exp, scale=0.5, r=[ba2], w=[ba2])
                k.tt("pool", ig[:, 0:tn], ig[:, 0:tn], xc[:, 0:tn], ALU.mult, r=[big, bxc], w=[big])
                k.tt("dve", a2[:, 0:tn], a2[:, 0:tn], ig[:, 0:tn], ALU.mult, r=[ba2, big], w=[ba2])
                ht_, bht = hrotB.next()
                if fwd:
                    k.scan(ht_[:, 0:tn], a_[:, 0:tn], a2[:, 0:tn], carry[:, n:n + 1], r=[ba_, ba2, b_carry], w=[bht])
                    k.cp("dve", carry[:, n:n + 1], ht_[:, tn - 1:tn], r=[bht], w=[b_carry])
                    k.dma("pool", oF[6 + n, :, t0:t0 + tn], ht_[:, 0:tn], r=[bht], w=[b_oF])
                else:
                    k.scan(ht_[:, 0:tn][:, ::-1], a_[:, 0:tn][:, ::-1], a2[:, 0:tn][:, ::-1], carry[:, n:n + 1], r=[ba_, ba2, b_carry], w=[bht])
                    k.cp("dve", carry[:, n:n + 1], ht_[:, 0:1], r=[bht], w=[b_carry])
                    hf, bhf = hrotB.next()
                    k.dma("sp", hf[:, 0:tn], oF[6 + n, :, t0:t0 + tn], r=[b_oF], w=[bhf])
                    k.tt("dve", ht_[:, 0:tn], ht_[:, 0:tn], hf[:, 0:tn], ALU.add, r=[bht, bhf], w=[bht])
                    sq, bsq = sqrot.next()
                    k.act(sq[:, 0:tn], ht_[:, 0:tn], AF.Square, r=[bht], w=[bsq])
                    k.mm(pss[:, 0:tn], ones, sq[:, 0:tn], start=(n == 0), stop=(n == 1), r=[bsq, b_ones], w=[bpss])
                    k.dma("sp", hf[:, 0:tn], sB[2 + n, :, t0:t0 + tn], r=[b_scr, bhf], w=[bhf])
                    k.stt("dve", ht_[:, 0:tn], ht_[:, 0:tn], gB[:, n:n + 1], hf[:, 0:tn], ALU.mult, ALU.mult, r=[bht, bgB, bhf], w=[bht])
                    k.dma("pool", hgB[n, :, t0:t0 + tn], ht_[:, 0:tn], r=[bht], w=[b_out])
            if not fwd:
                rs, brs_ = rsrot.next()
                k.cp("act", rs[0:1, 0:tn], pss[0:1, 0:tn], r=[bpss], w=[brs_])
                k.dma("pool", ssB[0:1, t0:t0 + tn], rs[0:1, 0:tn], r=[brs_], w=[b_out])
    return k.finish(["out"])


_PROGS = {}


def _prog(name):
    if name not in _PROGS:
        _PROGS[name] = {"ada": build_ada, "norm0": build_norm0, "m": build_m, "f": build_f}[name]()
    return _PROGS[name]


def _run(name, in_maps):
    res = run_bass_kernel_spmd(_prog(name), in_maps, core_ids=list(range(NCORES)))
    return res.results


def _c(a):
    return np.ascontiguousarray(a)


def _fm(v):
    v = np.asarray(v)
    n = v.size // D
    return _c(v.reshape(n, NCH, 128).transpose(2, 0, 1))


def _fm1(v):
    return _c(np.asarray(v).reshape(NCH, 128).T)


def _seg_gather(arr, j):
    out = np.zeros(arr.shape[:-1] + (TF,), arr.dtype)
    def put(o0, lo, hi, base, limit):
        a, b_ = max(lo, 0), min(hi, limit)
        out[..., o0 + (a - lo):o0 + (b_ - lo)] = arr[..., base + a:base + b_]
    put(0, j * 2048 - 64, j * 2048 + 1088, CTX, SEQ)
    put(SEG_L, j * 2048 + 960, j * 2048 + 2112, CTX, SEQ)
    put(2 * SEG_L, j * 64 - 1, j * 64 + 65, 0, CTX)
    return out


def _own_scatter(dst, src, j):
    dst[..., CTX + j * 2048:CTX + (j + 1) * 2048] = src[..., 0:2048]
    dst[..., j * 64:(j + 1) * 64] = src[..., 2048:2112]


def kernel(x, c, ctx, c_ctx, w_ada, b_ada, norm1_g, w_in, hgrn_lb_fwd, hgrn_lb_bwd, hgrn_norm_g,
           lru_conv_w, lru_conv_b, lru_wr_fwd, lru_br_fwd, lru_wi_fwd, lru_bi_fwd, lru_lam_fwd,
           lru_wr_bwd, lru_br_bwd, lru_wi_bwd, lru_bi_bwd, lru_lam_bwd, lru_norm_g,
           gla_wa_fwd, gla_ba_fwd, gla_wa_bwd, gla_ba_bwd, gla_norm_g, w_out, norm2_g,
           w_up, ffn_conv_w, ffn_conv_b, w_down, final_norm_g, _dbg=None):
    f32 = np.float32
    A = lambda a: np.asarray(a, dtype=f32)
    x, c, ctx, c_ctx = A(x), A(c), A(ctx), A(c_ctx)
    c3 = np.stack([c[0], c[1], c_ctx])
    cT = _c(c3.reshape(3, NCH, 128).transpose(2, 1, 0))
    w_ada = np.asarray(w_ada)
    res = _run("ada", [{"cT": cT, "w": _c(w_ada[:, :, i * ADA_COLS:(i + 1) * ADA_COLS]),
                        "b": _c(A(b_ada)[:, i * ADA_COLS:(i + 1) * ADA_COLS])} for i in range(NCORES)])
    mods = np.concatenate([r["mods"] for r in res], axis=2)
    zero_mod = np.zeros((128, 6, NCH), f32)
    mod_fm = [[_fm(mods[l, r]) for r in range(3)] for l in range(DEPTH)]
    if _dbg is not None:
        _dbg["mods"] = mods
    ins = []
    for i in range(NCORES):
        b, j = divmod(i, 4)
        xo = np.concatenate([x[b, j * 2048:(j + 1) * 2048], ctx[b, j * 64:(j + 1) * 64]], axis=0)
        ins.append({"xT": _c(xo.T.reshape(NCH, 128, TO)), "mod_lat": mod_fm[0][b], "mod_ctx": mod_fm[0][2],
                    "gn": _fm1(A(norm1_g)[0])})
    res = _run("norm0", ins)
    X = [np.zeros((NCH, 128, T_ALL), f32) for _ in range(BATCH)]
    H = [np.zeros((NCH, 128, T_ALL), ml_dtypes.bfloat16) for _ in range(BATCH)]
    for i in range(NCORES):
        b, j = divmod(i, 4)
        _own_scatter(X[b], ins[i]["xT"], j)
        _own_scatter(H[b], res[i]["hT"], j)
    w_in = np.asarray(w_in)
    out = np.zeros((BATCH, SEQ, D), f32)
    for l in range(DEPTH):
        wl = w_in[l]
        wk = lambda cols: _c(A(cols).reshape(NCH, 128, -1).transpose(1, 0, 2))
        ins = []
        for i in range(NCORES):
            b, j = divmod(i, 4)
            hs = [3 * j + h for h in range(3)]
            blk = [2 * j, 2 * j + 1]
            d = {"hT": H[b]}
            d["wA"] = np.stack([wk(np.concatenate([wl[:, hh * 128:(hh + 1) * 128], wl[:, 1536 + hh * 128:1536 + (hh + 1) * 128],
                                                    wl[:, 3072 + hh * 128:3072 + (hh + 1) * 128],
                                                    wl[:, 6144 + hh * 128:6144 + (hh + 1) * 128]], axis=1)) for hh in hs])
            d["wB"] = wk(np.concatenate([wl[:, 7680 + n * 128:7680 + (n + 1) * 128] for n in blk] +
                                        [wl[:, 8704 + n * 128:8704 + (n + 1) * 128] for n in blk], axis=1))
            d["wC1"] = wk(np.concatenate([wl[:, 9728 + j * 192:9728 + (j + 1) * 192], wl[:, 10496 + j * 192:10496 + (j + 1) * 192],
                                          wl[:, 14336:14352], wl[:, 14352:14368]], axis=1))
            d["wCg"] = wk(wl[:, 12800 + j * 384:12800 + (j + 1) * 384])
            d["wV"] = wk(np.concatenate([wl[:, 4608 + hh * 128:4608 + (hh + 1) * 128] for hh in hs] +
                                        [wl[:, 11264 + j * 384:11264 + (j + 1) * 384]], axis=1))
            d["lbf"] = _c(A(hgrn_lb_fwd).reshape(DEPTH, 12, 128)[:, hs, :].transpose(2, 1, 0))
            d["lbb"] = _c(A(hgrn_lb_bwd).reshape(DEPTH, 12, 128)[:, hs, :].transpose(2, 1, 0))
            lm = np.zeros((128, 4), f32)
            lm[:, 1:l + 1] = 1.0
            d["lmask"] = lm
            d["gA"] = _c(A(hgrn_norm_g)[l].reshape(12, 128)[hs].T)
            d["gC"] = _c(A(gla_norm_g)[l].reshape(12, 128)[3 * j:3 * j + 3].T)
            d["gB"] = _c(A(lru_norm_g)[l].reshape(8, 128)[blk].T)
            d["convwB"] = _c(A(lru_conv_w)[l].reshape(4, 8, 128)[:, blk, :].transpose(2, 1, 0))
            d["convbB"] = _c(A(lru_conv_b)[l].reshape(8, 128)[blk].T)
            for nm, arr in (("br_f", lru_br_fwd), ("bi_f", lru_bi_fwd), ("br_b", lru_br_bwd), ("bi_b", lru_bi_bwd),
                            ("lam_f", lru_lam_fwd), ("lam_b", lru_lam_bwd)):
                d[nm] = _c(A(arr)[l].reshape(8, 128)[blk].T)
            for nm, arr in (("wr_f", lru_wr_fwd), ("wi_f", lru_wi_fwd), ("wr_b", lru_wr_bwd), ("wi_b", lru_wi_bwd)):
                d[nm] = _c(A(arr)[l][blk].transpose(1, 0, 2))
            for nm, arr in (("wa_f", gla_wa_fwd), ("wa_b", gla_wa_bwd)):
                d[nm] = _c(A(arr)[l][:, j * 192:(j + 1) * 192])
            for nm, arr in (("ba_f", gla_ba_fwd), ("ba_b", gla_ba_bwd)):
                v = np.zeros((128, 2), f32)
                seg = A(arr)[l][j * 192:(j + 1) * 192]
                v[:, 0] = seg[0:128]
                v[0:64, 1] = seg[128:192]
                d[nm] = v
            ins.append(d)
        res = _run("m", ins)
        if _dbg is not None:
            _dbg["m%d" % l] = res
        wo_r = _c(A(w_out)[l].reshape(NCH, 128, NCH, 128).transpose(2, 1, 0, 3))
        wd_r = _c(A(w_down)[l].reshape(NCH, 128, NCH, 128).transpose(2, 1, 0, 3))
        wu_r = _c(A(w_up)[l].reshape(NCH, 128, 2, NCH, 128).transpose(3, 2, 1, 0, 4))
        convw = _c(A(ffn_conv_w)[l].reshape(9, NCH, 128).transpose(2, 1, 0))
        convb = _fm1(A(ffn_conv_b)[l])
        last = l == DEPTH - 1
        gn = _fm1(A(final_norm_g) if last else A(norm1_g)[l + 1])
        ins = []
        for b in range(BATCH):
            yAC = np.concatenate([res[b * 4 + j]["yA"] for j in range(4)] + [res[b * 4 + j]["yC"] for j in range(4)], axis=0)
            hgB = np.concatenate([res[b * 4 + j]["hgB"] for j in range(4)], axis=0)
            ssB = np.concatenate([res[b * 4 + j]["ssB"] for j in range(4)], axis=0)
            for j in range(4):
                hm = np.ones((128, 6), f32)
                if j == 0:
                    hm[:, 0] = 0.0
                    hm[:, 4] = 0.0
                if j == 3:
                    hm[:, 3] = 0.0
                    hm[:, 5] = 0.0
                ins.append({"xT": _seg_gather(X[b], j), "yAC": _seg_gather(yAC, j), "hgB": _seg_gather(hgB, j),
                            "ssB": _seg_gather(ssB, j), "w_out": wo_r, "w_up": wu_r, "w_down": wd_r,
                            "mod_lat": mod_fm[l][b], "mod_ctx": mod_fm[l][2],
                            "modn_lat": zero_mod if last else mod_fm[l + 1][b], "modn_ctx": zero_mod if last else mod_fm[l + 1][2],
                            "g2n": _fm1(A(norm2_g)[l]), "gn": gn, "convw": convw, "convb": convb, "hm": hm})
        res = _run("f", ins)
        if _dbg is not None:
            _dbg["f%d" % l] = res
        for i in range(NCORES):
            b, j = divmod(i, 4)
            _own_scatter(X[b], res[i]["xT_out"], j)
            _own_scatter(H[b], res[i]["hT_next"], j)
            if last:
                out[b, j * 2048:(j + 1) * 2048, :] = res[i]["fin"][:, :, 0:2048].reshape(D, 2048).T
        if _dbg is not None and _dbg.get("stop_after") == l:
            break
    return out
```
